# Optimizing a Trainium2 kernel written in Bass

```python
import math
import jax, jax.numpy as jnp
from jax import lax
import numpy as np

D_MODEL = 4096
BATCH = 8
SEQ = 2048
DEPTH = 2

GRID_W = 64
CTX_LEN = 256
N_MIXERS = 2
GROUP = 16
N_GROUPS = D_MODEL // GROUP
STATE = 64
CONV_W = 3
D_FF = (((8 * D_MODEL + 2) // 3 + 255) // 256) * 256
LN_EPS = 1e-5
ALPHA = (2.0 * DEPTH) ** 0.25
BETA = (8.0 * DEPTH) ** -0.25
N_LAYERS_A = (DEPTH + 1) // 2
N_LAYERS_B = DEPTH // 2
LOG_DT_MIN = math.log(1e-3)
LOG_DT_MAX = math.log(1e-1)

kernel_name = "hybrid_s5_shortconv_flow_backbone"


def _layer_norm(x, g, b):
    xf = x.astype(jnp.float32)
    mu = jnp.mean(xf, axis=-1, keepdims=True)
    var = jnp.mean(jnp.square(xf - mu), axis=-1, keepdims=True)
    return ((xf - mu) * lax.rsqrt(var + LN_EPS)).astype(x.dtype) * g + b


def _modulate(x, shift, scale):
    return x * (1.0 + scale) + shift


def _swiglu(h, w1, w3, w2):
    return (jax.nn.silu(h @ w1) * (h @ w3)) @ w2


def _zoh(lam_re, lam_im, log_step, b_re, b_im):
    dt = jnp.exp(log_step)[:, None]
    mag = jnp.exp(lam_re * dt)
    ang = lam_im * dt
    a_re = mag * jnp.cos(ang)
    a_im = mag * jnp.sin(ang)
    n_re, n_im = a_re - 1.0, a_im
    den = lam_re * lam_re + lam_im * lam_im
    f_re = (n_re * lam_re + n_im * lam_im) / den
    f_im = (n_im * lam_re - n_re * lam_im) / den
    bb_re = f_re[..., None] * b_re - f_im[..., None] * b_im
    bb_im = f_re[..., None] * b_im + f_im[..., None] * b_re
    return a_re, a_im, bb_re, bb_im


def _complex_scan(a_re, a_im, b_re, b_im):
    n = b_re.shape[0]
    a_re = jnp.broadcast_to(a_re, (n, 1) + a_re.shape)
    a_im = jnp.broadcast_to(a_im, (n, 1) + a_im.shape)

    def combine(e1, e2):
        a1r, a1i, b1r, b1i = e1
        a2r, a2i, b2r, b2i = e2
        return (a2r * a1r - a2i * a1i,
                a2r * a1i + a2i * a1r,
                a2r * b1r - a2i * b1i + b2r,
                a2r * b1i + a2i * b1r + b2i)

    _, _, s_re, s_im = lax.associative_scan(combine, (a_re, a_im, b_re, b_im), axis=0)
    return s_re, s_im


def _readout(s_re, s_im, c_re, c_im):
    return (jnp.einsum('nbgp,ghp->bngh', s_re, c_re)
            - jnp.einsum('nbgp,ghp->bngh', s_im, c_im))


def _s5_direction(ug_c, ug_l, lam_re, lam_im, log_step, b_re, b_im, c_re, c_im, ctx_out):
    a_re, a_im, bb_re, bb_im = _zoh(lam_re, lam_im, log_step, b_re, b_im)
    bc_re = jnp.einsum('bngh,gph->nbgp', ug_c, bb_re)
    bc_im = jnp.einsum('bngh,gph->nbgp', ug_c, bb_im)
    sc_re, sc_im = _complex_scan(a_re, a_im, bc_re, bc_im)
    h_re, h_im = sc_re[-1], sc_im[-1]
    bl_re = jnp.einsum('bngh,gph->nbgp', ug_l, bb_re)
    bl_im = jnp.einsum('bngh,gph->nbgp', ug_l, bb_im)
    bl_re = bl_re.at[0].add(a_re * h_re - a_im * h_im)
    bl_im = bl_im.at[0].add(a_re * h_im + a_im * h_re)
    sl_re, sl_im = _complex_scan(a_re, a_im, bl_re, bl_im)
    y_l = _readout(sl_re, sl_im, c_re, c_im)
    y_c = _readout(sc_re, sc_im, c_re, c_im) if ctx_out else None
    return y_c, y_l


def _glu_out(y, w_glu, w_out):
    z = jax.nn.gelu(y)
    return (z * jax.nn.sigmoid(z @ w_glu)) @ w_out


def _s5_mixer(h_c, h_l, w_in, lam_re, lam_im, log_step, b_re, b_im, c_re, c_im,
              d_skip, w_glu, w_out, ctx_out):
    bsz, n_lat, _ = h_l.shape
    n_ctx = h_c.shape[1]
    u_l = h_l @ w_in
    u_c = h_c @ w_in
    ug_l = u_l.reshape(bsz, n_lat, N_GROUPS, GROUP)
    ug_c = u_c.reshape(bsz, n_ctx, N_GROUPS, GROUP)
    y_l = d_skip * u_l
    y_c = d_skip * u_c if ctx_out else None
    for direction in range(2):
        flip = (lambda t: t[:, ::-1]) if direction == 1 else (lambda t: t)
        yc_d, yl_d = _s5_direction(flip(ug_c), flip(ug_l), lam_re[direction], lam_im[direction],
                                   log_step[direction], b_re[direction], b_im[direction],
                                   c_re[direction], c_im[direction], ctx_out)
        y_l = y_l + flip(yl_d).reshape(bsz, n_lat, D_MODEL)
        if ctx_out:
            y_c = y_c + flip(yc_d).reshape(bsz, n_ctx, D_MODEL)
    out_l = _glu_out(y_l, w_glu, w_out)
    out_c = _glu_out(y_c, w_glu, w_out) if ctx_out else None
    return out_c, out_l


def _conv3(v, w):
    pad = [(0, 0)] * (v.ndim - 2) + [(1, 1), (0, 0)]
    vp = jnp.pad(v, pad)
    return w[0] * vp[..., :-2, :] + w[1] * vp[..., 1:-1, :] + w[2] * vp[..., 2:, :]


def _conv_mixer(h_c, h_l, w_in, conv_w, w_out):
    bsz, n_lat, _ = h_l.shape
    rows = n_lat // GRID_W

    def gated(h):
        g_b, g_c, v = jnp.split(h @ w_in, 3, axis=-1)
        return g_b, g_c * v

    gb_l, z_l = gated(h_l)
    z_l = _conv3(z_l.reshape(bsz, rows, GRID_W, D_MODEL), conv_w).reshape(bsz, n_lat, D_MODEL)
    out_l = (gb_l * z_l) @ w_out
    out_c = None
    if h_c is not None:
        gb_c, z_c = gated(h_c)
        out_c = (gb_c * _conv3(z_c, conv_w)) @ w_out
    return out_c, out_l


def setup_inputs(seed: int = 0) -> dict:
    key = jax.random.key(seed)
    ks = jax.random.split(key, 26)
    D, G, P, H = D_MODEL, N_GROUPS, STATE, GROUP

    def nrm(k, shape, s):
        return jax.random.normal(k, shape, jnp.float32) * s

    x = nrm(ks[0], (BATCH, SEQ, D), 1.0)
    c = nrm(ks[1], (BATCH, D), 1.0)
    ctx = nrm(ks[2], (BATCH, CTX_LEN, D), 1.0)
    c_ctx = nrm(ks[3], (D,), 1.0)
    w_mod = nrm(ks[4], (DEPTH, D, 6 * D), 0.5 * D ** -0.5)
    b_mod = nrm(ks[5], (DEPTH, 6 * D), 0.02)
    ln_g = 1.0 + nrm(ks[6], (DEPTH, 2, D), 0.02)
    ln_b = nrm(ks[7], (DEPTH, 2, D), 0.02)
    w_in_a = nrm(ks[8], (N_LAYERS_A, D, D), D ** -0.5)
    lam_re = -0.5 + nrm(ks[9], (N_LAYERS_A, 2, G, P), 0.01)
    lam_im = jnp.pi * jnp.arange(P, dtype=jnp.float32) + nrm(ks[10], (N_LAYERS_A, 2, G, P), 0.01)
    log_step = jax.random.uniform(ks[11], (N_LAYERS_A, 2, G), jnp.float32, LOG_DT_MIN, LOG_DT_MAX)
    b_re = nrm(ks[12], (N_LAYERS_A, 2, G, P, H), (2.0 * H) ** -0.5)
    b_im = nrm(ks[13], (N_LAYERS_A, 2, G, P, H), (2.0 * H) ** -0.5)
    c_re = nrm(ks[14], (N_LAYERS_A, 2, G, H, P), P ** -0.5)
    c_im = nrm(ks[15], (N_LAYERS_A, 2, G, H, P), P ** -0.5)
    d_skip = nrm(ks[16], (N_LAYERS_A, D), 1.0)
    w_glu_a = nrm(ks[17], (N_LAYERS_A, D, D), D ** -0.5)
    w_out_a = nrm(ks[18], (N_LAYERS_A, D, D), BETA * D ** -0.5)
    w_in_b = nrm(ks[19], (N_LAYERS_B, D, 3 * D), D ** -0.5)
    conv_w = nrm(ks[20], (N_LAYERS_B, CONV_W, D), CONV_W ** -0.5)
    w_out_b = nrm(ks[21], (N_LAYERS_B, D, D), BETA * D ** -0.5)
    w1 = nrm(ks[22], (DEPTH, D, D_FF), D ** -0.5)
    w3 = nrm(ks[23], (DEPTH, D, D_FF), D ** -0.5)
    w2 = nrm(ks[24], (DEPTH, D_FF, D), BETA * D_FF ** -0.5)
    return {"x": x, "c": c, "ctx": ctx, "c_ctx": c_ctx, "w_mod": w_mod, "b_mod": b_mod,
            "ln_g": ln_g, "ln_b": ln_b, "w_in_a": w_in_a, "lam_re": lam_re, "lam_im": lam_im,
            "log_step": log_step, "b_re": b_re, "b_im": b_im, "c_re": c_re, "c_im": c_im,
            "d_skip": d_skip, "w_glu_a": w_glu_a, "w_out_a": w_out_a, "w_in_b": w_in_b,
            "conv_w": conv_w, "w_out_b": w_out_b, "w1": w1, "w3": w3, "w2": w2}


def reference(x, c, ctx, c_ctx, w_mod, b_mod, ln_g, ln_b, w_in_a, lam_re, lam_im, log_step,
              b_re, b_im, c_re, c_im, d_skip, w_glu_a, w_out_a, w_in_b, conv_w, w_out_b,
              w1, w3, w2):
    xl, xc = x, ctx
    silu_c = jax.nn.silu(c)
    silu_cc = jax.nn.silu(c_ctx)
    for i in range(DEPTH):
        kind = i % N_MIXERS
        j = i // N_MIXERS
        ctx_later = any(k % N_MIXERS == 0 for k in range(i + 1, DEPTH))
        mod = (silu_c @ w_mod[i] + b_mod[i])[:, None, :]
        sh_m, sc_m, g_m, sh_f, sc_f, g_f = jnp.split(mod, 6, axis=-1)
        h_l = _modulate(xl, sh_m, sc_m)
        h_c = None
        if kind == 0 or ctx_later:
            mod_c = silu_cc @ w_mod[i] + b_mod[i]
            csh_m, csc_m, cg_m, csh_f, csc_f, cg_f = jnp.split(mod_c, 6, axis=-1)
            h_c = _modulate(xc, csh_m, csc_m)
        if kind == 0:
            o_c, o_l = _s5_mixer(h_c, h_l, w_in_a[j], lam_re[j], lam_im[j], log_step[j],
                                 b_re[j], b_im[j], c_re[j], c_im[j], d_skip[j],
                                 w_glu_a[j], w_out_a[j], ctx_later)
        else:
            o_c, o_l = _conv_mixer(h_c if ctx_later else None, h_l, w_in_b[j], conv_w[j], w_out_b[j])
        xl = _layer_norm(ALPHA * xl + g_m * o_l, ln_g[i, 0], ln_b[i, 0])
        xl = _layer_norm(ALPHA * xl + g_f * _swiglu(_modulate(xl, sh_f, sc_f), w1[i], w3[i], w2[i]),
                         ln_g[i, 1], ln_b[i, 1])
        if ctx_later:
            xc = _layer_norm(ALPHA * xc + cg_m * o_c, ln_g[i, 0], ln_b[i, 0])
            xc = _layer_norm(ALPHA * xc + cg_f * _swiglu(_modulate(xc, csh_f, csc_f), w1[i], w3[i], w2[i]),
                             ln_g[i, 1], ln_b[i, 1])
    return xl
```

```python
import math
import numpy as np
import concourse.bass as bass
import concourse.mybir as mybir
from concourse.bass_utils import run_bass_kernel_spmd

F32 = mybir.dt.float32
BF16 = mybir.dt.bfloat16
I32 = mybir.dt.int32
AF = mybir.ActivationFunctionType
ALU = mybir.AluOpType

D = 4096
KC = 32
TL = 2048
TC = 256
FF = 11008
FC = 86
NT = 4
ALPHA = 4.0 ** 0.25
EPS = 1e-5
NCH = 288
GB = 16
NBLK = 256 // GB
NPR = GB // 2
DBG = False
STOP = None
SIMQ = False
CUT = 0
NOXA = False
SMALL_SHAPES = {"w_mod": [2, 4096, 512], "w_glu_a": [1, 4096, 128], "w_out_a": [1, 4096, 128], "w_in_b": [1, 4096, 128],
                "w_out_b": [1, 4096, 128], "w1": [2, 4096, 128], "w3": [2, 4096, 128], "w2": [2, 128, 4096]}


class Res:
    __slots__ = ("name", "w", "r", "sem", "excl", "nobar")

    def __init__(self, name, excl=False, nobar=False):
        self.name = name
        self.excl = excl
        self.nobar = nobar
        self.w = None
        self.r = {}
        self.sem = None


class Prog:
    ENGS = ("pe", "act", "dve", "pool", "sp")

    def __init__(self, nc):
        self.nc = nc
        self.streams = {e: [] for e in self.ENGS}
        self.marked = {e: set() for e in self.ENGS}
        self.nseq = {e: 0 for e in self.ENGS}
        self.esem = {e: nc.alloc_semaphore("es_" + e) for e in self.ENGS}
        self.dsems = []
        self.dcount = []
        self.nobar_sems = set()

    def _deps(self, reads, writes):
        deps = []
        for r in reads:
            if r.w is not None:
                deps.append(r.w)
        for w in writes:
            if w.w is not None:
                deps.append(w.w)
            deps.extend(w.r.values())
        return deps

    def _commit(self, ev, reads, writes):
        for r in reads:
            r.r[(ev[0], ev[1])] = ev
        for w in writes:
            w.w = ev
            w.r = {}

    def _mark(self, deps, eng):
        out = []
        for d in deps:
            if d[0] == "E":
                if d[1] == eng and eng == "pe":
                    continue
                self.marked[d[1]].add(d[2])
            out.append(d)
        return out

    def op(self, eng, fn, reads=(), writes=()):
        ex = [r for r in reads if r.excl]
        if ex:
            writes = list(writes) + [r for r in ex if r not in writes]
            reads = [r for r in reads if not r.excl]
        deps = self._mark(self._deps(reads, writes), eng)
        seq = self.nseq[eng]
        self.nseq[eng] += 1
        self.streams[eng].append(("op", fn, deps, seq))
        self._commit(("E", eng, seq), reads, writes)

    def dma(self, q, fn, reads=(), writes=(), semres=None):
        deps = self._mark(self._deps(reads, writes), q)
        if semres is None:
            semres = writes[0] if writes else reads[0]
        if semres.sem is None:
            semres.sem = len(self.dsems)
            self.dsems.append(self.nc.alloc_semaphore("ds_%d" % semres.sem))
            self.dcount.append(0)
        si = semres.sem
        if semres.nobar:
            self.nobar_sems.add(si)
        self.dcount[si] += 16
        self.streams[q].append(("dma", fn, deps, si))
        self._commit(("D", si, self.dcount[si]), reads, writes)

    def barrier(self):
        evs = []
        for e in self.ENGS:
            if self.nseq[e] > 0:
                self.marked[e].add(self.nseq[e] - 1)
                evs.append(("E", e, self.nseq[e] - 1))
        for si in range(len(self.dsems)):
            if self.dcount[si] > 0 and si not in self.nobar_sems:
                evs.append(("D", si, self.dcount[si]))
        for e in self.ENGS:
            self.streams[e].append(("bar", None, list(evs), None))

    def emit(self):
        nc = self.nc
        rank = {}
        for e in self.ENGS:
            rank[e] = {s: i + 1 for i, s in enumerate(sorted(self.marked[e]))}
        prog = self

        def run(eng_name, engobj):
            known = {}
            for kind, fn, deps, x in prog.streams[eng_name]:
                for d in deps:
                    if d[0] == "E":
                        sem = prog.esem[d[1]]
                        val = rank[d[1]][d[2]]
                        key = ("E", d[1])
                    else:
                        sem = prog.dsems[d[1]]
                        val = d[2]
                        key = ("D", d[1])
                    if known.get(key, 0) >= val:
                        continue
                    engobj.wait_ge(sem, val)
                    known[key] = val
                if kind == "bar":
                    continue
                ins = fn(engobj)
                if kind == "op":
                    if x in prog.marked[eng_name]:
                        ins.then_inc(prog.esem[eng_name], 1)
                else:
                    ins.then_inc(prog.dsems[x], 16)
            if eng_name == "sp":
                for si, sem in enumerate(prog.dsems):
                    if prog.dcount[si] > 0 and known.get(("D", si), 0) < prog.dcount[si]:
                        engobj.wait_ge(sem, prog.dcount[si])
                for e in prog.ENGS:
                    n = len(rank[e])
                    if e != "sp" and n > 0 and known.get(("E", e), 0) < n:
                        engobj.wait_ge(prog.esem[e], n)

        with nc.Block() as block:
            @block.tensor
            def _(eng):
                run("pe", eng)

            @block.scalar
            def _(eng):
                run("act", eng)

            @block.vector
            def _(eng):
                run("dve", eng)

            @block.gpsimd
            def _(eng):
                run("pool", eng)

            @block.sync
            def _(eng):
                run("sp", eng)


def build():
    nc = bass.Bass("TRN2", target_bir_lowering=False)

    def din(name, shape):
        if SIMQ and name in SMALL_SHAPES:
            shape = SMALL_SHAPES[name]
        return nc.dram_tensor(name, shape, F32, kind="ExternalInput").ap()

    x = din("x", [1, TL, D]); c_in = din("c", [1, D]); ctx = din("ctx", [1, TC, D]); c_ctx = din("c_ctx", [D])
    w_mod = din("w_mod", [2, D, 6 * D]); b_mod = din("b_mod", [2, 6 * D])
    ln_g = din("ln_g", [2, 2, D]); ln_b = din("ln_b", [2, 2, D])
    w_in_a = din("w_in_a", [1, D, D]); lam_re = din("lam_re", [1, 2, 256, 64]); lam_im = din("lam_im", [1, 2, 256, 64])
    log_step = din("log_step", [1, 2, 256]); b_re = din("b_re", [1, 2, 256, 64, 16]); b_im = din("b_im", [1, 2, 256, 64, 16])
    c_re = din("c_re", [1, 2, 256, 16, 64]); c_im = din("c_im", [1, 2, 256, 16, 64]); d_skip = din("d_skip", [1, D])
    w_glu_a = din("w_glu_a", [1, D, D]); w_out_a = din("w_out_a", [1, D, D]); w_in_b = din("w_in_b", [1, D, 3 * D])
    conv_w = din("conv_w", [1, 3, D]); w_out_b = din("w_out_b", [1, D, D])
    w1 = din("w1", [2, D, FF]); w3 = din("w3", [2, D, FF]); w2 = din("w2", [2, FF, D])
    out = nc.dram_tensor("out", [1, TL, D], F32, kind="ExternalOutput").ap()

    skind = "ExternalOutput" if DBG else "Internal"
    Uscr = nc.dram_tensor("Uscr", [NCH, 256, 128], F32, kind=skind).ap()
    XAscr = nc.dram_tensor("XAscr", [NT, 128, KC * 512], F32, kind=skind).ap()
    Zscr = nc.dram_tensor("Zscr", [256, 8, D], BF16, kind=skind).ap()
    modrow = nc.dram_tensor("modrow", [3, 6 * D], F32, kind=skind).ap()

    P = Prog(nc)
    A = lambda f: (lambda e: f(e))

    identf = nc.alloc_sbuf_tensor("identf", [128, 128], F32)
    identb = nc.alloc_sbuf_tensor("identb", [128, 128], BF16)
    onesD = nc.alloc_sbuf_tensor("onesD", [128, 128], F32)
    mvraw = nc.alloc_sbuf_tensor("mvraw", [128, 3, 192], F32)
    braw = nc.alloc_sbuf_tensor("braw", [128, 3, 192], F32)
    dv = nc.alloc_sbuf_tensor("dv", [128, 24, 32], F32)
    lng = nc.alloc_sbuf_tensor("lng", [128, 4, 32], F32)
    lnb = nc.alloc_sbuf_tensor("lnb", [128, 4, 32], F32)
    cwv = nc.alloc_sbuf_tensor("cwv", [128, 3, 32], F32)
    scbf = nc.alloc_sbuf_tensor("scbf", [128, 32, 2], BF16)
    cs = nc.alloc_sbuf_tensor("cs", [128, 32, 2], F32)
    maskf = nc.alloc_sbuf_tensor("maskf", [128, 2, 128], F32)
    ARENA_W = 50000
    arena = nc.alloc_sbuf_tensor("arena", [128, ARENA_W], F32)
    R_const = Res("const")
    R_mv = Res("mv")

    def carve(off, shape, dtype=F32, parts=128):
        n = 1
        for s in shape:
            n *= s
        words = n if dtype in (F32, I32) else (n + 1) // 2
        assert off + words <= ARENA_W, (off, words)
        ap = arena[0:parts, off:off + words]
        if dtype != F32:
            ap = ap.bitcast(dtype)
        if len(shape) == 2:
            ap = ap.rearrange("p (a b) -> p a b", a=shape[0])
        elif len(shape) == 3:
            ap = ap.rearrange("p (a b c) -> p a b c", a=shape[0], b=shape[1])
        elif len(shape) == 4:
            ap = ap.rearrange("p (a b c d) -> p a b c d", a=shape[0], b=shape[1], c=shape[2])
        return ap, off + words

    pbank = [nc.alloc_psum_tensor("pb%d" % i, [128, 512], F32) for i in range(8)]
    R_pb = [Res("pb%d" % i, excl=True) for i in range(8)]

    P.op("pool", lambda e: e.memset(identf[:], 1.0), writes=[R_const])
    P.op("pool", lambda e: e.affine_select(out=identf[:], in_=identf[:], pattern=[[-1, 128]], compare_op=ALU.is_equal,
                                            fill=0.0, base=0, channel_multiplier=1), reads=[R_const], writes=[R_const])
    P.op("dve", lambda e: e.tensor_copy(out=identb[:], in_=identf[:]), reads=[R_const], writes=[R_const])
    P.op("pool", lambda e: e.memset(onesD[:], 1.0 / D), writes=[R_const])
    P.op("pool", lambda e: e.memset(maskf[:], 1.0), writes=[R_const])
    P.op("pool", lambda e: e.affine_select(out=maskf[:, 0, :], in_=maskf[:, 0, :], pattern=[[16, 8], [0, 16]],
                                            compare_op=ALU.is_ge, fill=0.0, base=15, channel_multiplier=-1),
         reads=[R_const], writes=[R_const])
    P.op("pool", lambda e: e.affine_select(out=maskf[:, 1, :], in_=maskf[:, 1, :], pattern=[[-16, 8], [0, 16]],
                                            compare_op=ALU.is_ge, fill=0.0, base=0, channel_multiplier=1),
         reads=[R_const], writes=[R_const])

    def sdma(out_ap, in_ap, writes):
        P.dma("sp", lambda e: e.dma_start(out=out_ap, in_=in_ap, allow_slow_non_contiguous=True), writes=writes)

    sdma(cs[:, :, 0], c_in[0].rearrange("(k p) -> p k", p=128), [R_mv])
    sdma(cs[:, :, 1], c_ctx.rearrange("(k p) -> p k", p=128), [R_mv])
    for r, l in ((0, 0), (1, 0), (2, 1)):
        sdma(braw[:, r, :], b_mod[l].rearrange("(j p) -> p j", p=128), [R_mv])
    for l in range(2):
        for s in range(2):
            sdma(lng[:, l * 2 + s, :], ln_g[l, s].rearrange("(k p) -> p k", p=128), [R_mv])
            sdma(lnb[:, l * 2 + s, :], ln_b[l, s].rearrange("(k p) -> p k", p=128), [R_mv])
    for t in range(3):
        sdma(cwv[:, t, :], conv_w[0, t].rearrange("(k p) -> p k", p=128), [R_mv])
    P.op("act", lambda e: e.activation(out=cs[:], in_=cs[:], func=AF.Silu), reads=[R_mv], writes=[R_mv])

    off = 0
    NWB = 4
    wts = []
    for i in range(NWB):
        ap, off = carve(off, [32, 128], BF16)
        wts.append(ap)
    R_wt = [Res("wt%d" % i) for i in range(NWB)]
    wt_ctr = [0]
    ARENA0 = off

    def convert(name, W):
        din_, dout_ = W.shape
        nk, nm = din_ // 128, dout_ // 128
        Wb = nc.dram_tensor(name + "_bf", [nm, 128, nk * 128], BF16).ap()
        r = Res(name + "_bf", nobar=True)
        for m in range(nm):
            P.dma("pool", lambda e, m=m: e.dma_start(out=Wb[m].rearrange("p (k n) -> p k n", n=128),
                                                    in_=W[:, m * 128:(m + 1) * 128].rearrange("(k p) n -> p k n", p=128)), writes=[r])
        return (Wb, r)

    CV = {}
    CV["w_in_a"] = convert("w_in_a", w_in_a[0])
    if not SIMQ:
        CV["w_glu_a"] = convert("w_glu_a", w_glu_a[0])
        CV["w_out_a"] = convert("w_out_a", w_out_a[0])
        CV["w1_0"] = convert("w1_0", w1[0]); CV["w3_0"] = convert("w3_0", w3[0]); CV["w2_0"] = convert("w2_0", w2[0])
        CV["w_in_b"] = convert("w_in_b", w_in_b[0]); CV["w_out_b"] = convert("w_out_b", w_out_b[0])
        CV["w1_1"] = convert("w1_1", w1[1]); CV["w3_1"] = convert("w3_1", w3[1]); CV["w2_1"] = convert("w2_1", w2[1])

    def load_w(cv, m, k0, nk):
        Wb, rcv = cv
        i = wt_ctr[0] % NWB
        wt_ctr[0] += 1
        t = wts[i][:, 0:nk, :]
        P.dma("sp", lambda e: e.dma_start(out=t, in_=Wb[m][:, k0 * 128:(k0 + nk) * 128].rearrange("p (k n) -> p k n", n=128)),
              reads=[rcv], writes=[R_wt[i]], semres=R_wt[i])
        return t, R_wt[i]

    off = ARENA0
    wmf = []
    for i in range(2):
        ap, off = carve(off, [32, 512], F32)
        wmf.append(ap)
    rts = []
    for i in range(2):
        ap, off = carve(off, [512], F32, parts=2)
        rts.append(ap)
    R_wmf = [Res("wmf0"), Res("wmf1")]
    R_rt = [Res("rt0"), Res("rt1")]
    R_modrow = Res("modrow")
    cnt = 0
    for l in range(2):
        for nb in range(1 if SIMQ else 48):
            wb_ = cnt % 2
            q = "sp" if cnt % 2 == 0 else "act"
            P.dma(q, lambda e, l=l, nb=nb, wb_=wb_: e.dma_start(out=wmf[wb_][:], in_=w_mod[l, :, nb * 512:(nb + 1) * 512].rearrange("(k p) n -> p k n", p=128)),
                  writes=[R_wmf[wb_]])
            pb = cnt % 2
            for k in range(KC):
                P.op("pe", lambda e, k=k, wb_=wb_, pb=pb: e.matmul(out=pbank[pb][0:2, :], lhsT=cs[:, k, :], rhs=wmf[wb_][:, k, :],
                                                               start=(k == 0), stop=(k == KC - 1)),
                     reads=[R_wmf[wb_], R_mv], writes=[R_pb[pb]])
            rt = rts[cnt % 2]
            P.op("act", lambda e, rt=rt, pb=pb: e.activation(out=rt[:, :], in_=pbank[pb][0:2, :], func=AF.Copy),
                 reads=[R_pb[pb]], writes=[R_rt[cnt % 2]])
            if l == 0:
                P.dma("sp", lambda e, rt=rt, nb=nb: e.dma_start(out=modrow[0:2, nb * 512:(nb + 1) * 512], in_=rt[0:2, :]),
                      reads=[R_rt[cnt % 2]], writes=[R_modrow], semres=R_rt[cnt % 2])
            else:
                P.dma("sp", lambda e, rt=rt, nb=nb: e.dma_start(out=modrow[2:3, nb * 512:(nb + 1) * 512], in_=rt[0:1, :]),
                      reads=[R_rt[cnt % 2]], writes=[R_modrow], semres=R_rt[cnt % 2])
            cnt += 1
    if SIMQ:
        P.op("pool", lambda e: e.memset(mvraw[:], 0.1), reads=[R_modrow], writes=[R_mv])
    for r in range(0 if SIMQ else 3):
        P.dma("sp", lambda e, r=r: e.dma_start(out=mvraw[:, r, :], in_=modrow[r].rearrange("(j p) -> p j", p=128),
                                             allow_slow_non_contiguous=True), reads=[R_modrow], writes=[R_mv])
    P.op("dve", lambda e: e.tensor_tensor(out=mvraw[:], in0=mvraw[:], in1=braw[:], op=ALU.add), reads=[R_mv], writes=[R_mv])

    def MV(r, t):
        return mvraw[:, r, t * 32:(t + 1) * 32]
    def TSop(out, in0, s1, s2, op0, op1):
        P.op("dve", lambda e: e.tensor_scalar(out=out, in0=in0, scalar1=s1, scalar2=s2, op0=op0, op1=op1), reads=[R_mv], writes=[R_mv])
    def TTop(out, in0, in1, op):
        P.op("dve", lambda e: e.tensor_tensor(out=out, in0=in0, in1=in1, op=op), reads=[R_mv], writes=[R_mv])
    TSop(dv[:, 0, :], MV(0, 1), 1.0, None, ALU.add, ALU.bypass)
    TSop(dv[:, 1, :], MV(1, 1), 1.0, None, ALU.add, ALU.bypass)
    nxt = [(0, 4, 3), (2, 1, 0), (2, 4, 3), None]
    for n in range(4):
        ab, hs, hb = dv[:, 4 + n * 3, :], dv[:, 5 + n * 3, :], dv[:, 6 + n * 3, :]
        if n < 3:
            TSop(ab, lnb[:, n, :], ALPHA, None, ALU.mult, ALU.bypass)
            r, sci, shi = nxt[n]
            TSop(hs, MV(r, sci), 1.0, None, ALU.add, ALU.bypass)
            TTop(hb, lnb[:, n, :], hs, ALU.mult)
            TTop(hb, hb, MV(r, shi), ALU.add)
        else:
            TSop(ab, lnb[:, n, :], 1.0, None, ALU.mult, ALU.bypass)
    P.barrier()
    if STOP == 'M':
        P.emit(); return nc

    off = ARENA0
    X2 = []
    for i in range(2):
        ap, off = carve(off, [D], F32, parts=64)
        X2.append(ap)
    hT, off = carve(off, [KC, 512], BF16)
    xaS, off = carve(off, [KC, 512], F32)
    U2s = []
    for i in range(2):
        ap, off = carve(off, [8, 8, 16], F32, parts=64)
        U2s.append(ap)
    R_X2 = [Res("X2a"), Res("X2b")]
    R_hT = Res("hT"); R_xaS = Res("xaS"); R_U2s = [Res("U2s0"), Res("U2s1")]
    R_Uscr = Res("Uscr"); R_XAscr = Res("XAscr")
    xcnt = 0
    pcnt = 0
    ucnt = 0
    seqs = [(ctx[0], 0, 32, 0, None)] + [(x[0], tt * 512, 64, 32 + tt * 64, tt) for tt in range(NT)]
    for src, base, ncl, c0, tt in (seqs[:2] if SIMQ else seqs):
        if CUT == 10 and tt is not None:
            P.emit(); return nc
        rows = src[base:base + 8 * ncl, :].rearrange("(cl i) d -> i cl d", i=8)
        for i in range(8):
            xb = xcnt % 2
            xcnt += 1
            P.dma("sp", lambda e, xb=xb, i=i, rows=rows, ncl=ncl: e.dma_start(out=X2[xb][0:ncl, :], in_=rows[i]), writes=[R_X2[xb]])
            if CUT == 1:
                P.emit(); return nc
            for k8 in range(4):
                pb = 2 + (pcnt % 2)
                pcnt += 1
                pv = pbank[pb][:].rearrange("p (a b) -> p a b", a=8)
                for kk in range(8):
                    k = k8 * 8 + kk
                    P.op("pe", lambda e, pv=pv, kk=kk, k=k, xb=xb, ncl=ncl: e.transpose(
                        out=pv[:, kk, 0:ncl], in_=X2[xb][0:ncl, k * 128:(k + 1) * 128], identity=identf[0:ncl, 0:ncl]),
                        reads=[R_X2[xb], R_const], writes=[R_pb[pb]])
                if CUT == 2:
                    P.emit(); return nc
                for kk in range(8):
                    k = k8 * 8 + kk
                    if tt is None:
                        s_ap, b_ap = dv[:, 1, k:k + 1], mvraw[:, 1, k:k + 1]
                    else:
                        s_ap, b_ap = dv[:, 0, k:k + 1], mvraw[:, 0, k:k + 1]
                    P.op("act", lambda e, pv=pv, kk=kk, k=k, i=i, ncl=ncl, s_ap=s_ap, b_ap=b_ap: e.activation(
                        out=hT[:, k, i * ncl:(i + 1) * ncl], in_=pv[:, kk, 0:ncl], func=AF.Identity, bias=b_ap, scale=s_ap),
                        reads=[R_pb[pb], R_mv], writes=[R_hT])
                if CUT == 3:
                    P.emit(); return nc
                if tt is not None and not NOXA:
                    P.op("dve", lambda e, pv=pv, k8=k8, i=i, ncl=ncl: e.tensor_scalar(
                        out=xaS[:, k8 * 8:(k8 + 1) * 8, i * ncl:(i + 1) * ncl], in0=pv[:, :, 0:ncl], scalar1=ALPHA, scalar2=None,
                        op0=ALU.mult, op1=ALU.bypass), reads=[R_pb[pb]], writes=[R_xaS])
        if CUT == 9 and tt is not None:
            P.emit(); return nc
        if tt is not None:
            P.dma("sp", lambda e, tt=tt: e.dma_start(out=XAscr[tt], in_=xaS[:].rearrange("p a b -> p (a b)"), max_dma_last_dim=16384),
                  reads=[R_xaS], writes=[R_XAscr], semres=R_xaS)
        if CUT == 4 or (CUT == 8 and tt is not None):
            P.emit(); return nc
        for nb in range(2 if SIMQ else 32):
            wt, rw = load_w(CV["w_in_a"], nb, 0, 32)
            ub = ucnt % 2
            ucnt += 1
            for i in range(8):
                pb = 4 + (i % 2)
                for k in range(KC):
                    P.op("pe", lambda e, pb=pb, k=k, i=i, ncl=ncl, wt=wt: e.matmul(
                        out=pbank[pb][0:ncl, 0:128], lhsT=hT[:, k, i * ncl:(i + 1) * ncl], rhs=wt[:, k, :],
                        start=(k == 0), stop=(k == KC - 1)), reads=[R_hT, rw], writes=[R_pb[pb]])
                if CUT == 5:
                    P.emit(); return nc
                eng = "act" if i % 2 == 0 else "dve"
                if eng == "act":
                    P.op("act", lambda e, pb=pb, ub=ub, i=i, ncl=ncl: e.activation(out=U2s[ub][0:ncl, :, i, :], in_=pbank[pb][0:ncl, 0:128].rearrange("p (g h) -> p g h", g=8), func=AF.Copy),
                         reads=[R_pb[pb]], writes=[R_U2s[ub]])
                else:
                    P.op("dve", lambda e, pb=pb, ub=ub, i=i, ncl=ncl: e.tensor_copy(out=U2s[ub][0:ncl, :, i, :], in_=pbank[pb][0:ncl, 0:128].rearrange("p (g h) -> p g h", g=8)),
                         reads=[R_pb[pb]], writes=[R_U2s[ub]])
            if CUT == 6:
                P.emit(); return nc
            P.dma("sp", lambda e, ub=ub, nb=nb, c0=c0, ncl=ncl: e.dma_start(out=Uscr[c0:c0 + ncl, nb * 8:(nb + 1) * 8, :], in_=U2s[ub][0:ncl, :, :, :].rearrange("p g j h -> p g (j h)")),
                  reads=[R_U2s[ub]], writes=[R_Uscr], semres=R_U2s[ub])
            if CUT == 7:
                P.emit(); return nc
    P.barrier()
    if STOP == '0A':
        P.emit(); return nc

    off = 0
    def T256(n=1):
        nonlocal off
        ap, off = carve(off, [n, 256] if n > 1 else [256], F32)
        return ap
    Ere = T256(9); Eim = T256(9); Nre = T256(8); Nim = T256(8); fre = T256(); fim = T256()
    PERS_END = off
    lr = T256(); li = T256(); ls = T256(); t0 = T256(); t1 = T256(); t2 = T256(); t3 = T256(); t4 = T256(); t5 = T256()
    ki, off = carve(off, [256], I32)
    R_S = Res("S5glob")

    def dve(fn, reads=(R_S,), writes=(R_S,)):
        P.op("dve", fn, reads=list(reads), writes=list(writes))

    def act(fn, reads=(R_S,), writes=(R_S,)):
        P.op("act", fn, reads=list(reads), writes=list(writes))

    for g2 in range(2):
        sl = slice(g2 * 64, (g2 + 1) * 64)
        sdma(lr[sl, :].rearrange("p (d pr) -> p d pr", d=2), lam_re[0].rearrange("d (pr g2) p -> g2 p d pr", g2=2)[g2], [R_S])
        sdma(li[sl, :].rearrange("p (d pr) -> p d pr", d=2), lam_im[0].rearrange("d (pr g2) p -> g2 p d pr", g2=2)[g2], [R_S])
        sdma(ls[sl, :].rearrange("p (d pr) -> p d pr", d=2),
             log_step[0].rearrange("d (pr g2) -> g2 d pr", g2=2)[g2].partition_broadcast(64), [R_S])

    def tt_(out, a, b, op):
        dve(lambda e: e.tensor_tensor(out=out, in0=a, in1=b, op=op))

    def ts_(out, a, s1, s2, op0, op1=ALU.bypass):
        dve(lambda e: e.tensor_scalar(out=out, in0=a, scalar1=s1, scalar2=s2, op0=op0, op1=op1))

    def cmul(ore, oim, are, aim, bre, bim, ta, tb):
        tt_(ta, aim, bim, ALU.mult)
        tt_(tb, aim, bre, ALU.mult)
        tt_(ore, are, bre, ALU.mult)
        tt_(ore, ore, ta, ALU.subtract)
        tt_(oim, are, bim, ALU.mult)
        tt_(oim, oim, tb, ALU.add)

    act(lambda e: e.activation(out=ls, in_=ls, func=AF.Exp))
    tt_(t0, lr, ls, ALU.mult)
    act(lambda e: e.activation(out=t0, in_=t0, func=AF.Exp))
    tt_(t1, li, ls, ALU.mult)
    ts_(t2, t1, 1.0 / (2 * math.pi), None, ALU.mult)
    dve(lambda e: e.tensor_copy(out=ki, in_=t2))
    dve(lambda e: e.tensor_copy(out=t2, in_=ki))
    dve(lambda e: e.scalar_tensor_tensor(out=t1, in0=t2, scalar=-2 * math.pi, in1=t1, op0=ALU.mult, op1=ALU.add))
    halfpi, off = carve(off, [1], F32)
    dve(lambda e: e.memset(halfpi, math.pi / 2))
    act(lambda e: e.activation(out=t2, in_=t1, func=AF.Sin, scale=0.25))
    act(lambda e: e.activation(out=t3, in_=t1, func=AF.Sin, scale=0.25, bias=halfpi[:, 0:1]))
    for _ in range(2):
        tt_(t4, t2, t3, ALU.mult)
        tt_(t5, t2, t2, ALU.mult)
        ts_(t2, t4, 2.0, None, ALU.mult)
        ts_(t3, t5, -2.0, 1.0, ALU.mult, ALU.add)
    tt_(Ere[:, 1, :], t0, t3, ALU.mult)
    tt_(Eim[:, 1, :], t0, t2, ALU.mult)
    dve(lambda e: e.memset(Ere[:, 0, :], 1.0)); dve(lambda e: e.memset(Eim[:, 0, :], 0.0))
    dve(lambda e: e.memset(Nre[:, 0, :], 1.0)); dve(lambda e: e.memset(Nim[:, 0, :], 0.0))
    for k in range(2, 9):
        cmul(Ere[:, k, :], Eim[:, k, :], Ere[:, k - 1, :], Eim[:, k - 1, :], Ere[:, 1, :], Eim[:, 1, :], t4, t5)
    tt_(t4, t0, t0, ALU.mult)
    dve(lambda e: e.reciprocal(out=t4, in_=t4))
    tt_(Nre[:, 1, :], Ere[:, 1, :], t4, ALU.mult)
    tt_(Nim[:, 1, :], Eim[:, 1, :], t4, ALU.mult)
    ts_(Nim[:, 1, :], Nim[:, 1, :], -1.0, None, ALU.mult)
    for k in range(2, 8):
        cmul(Nre[:, k, :], Nim[:, k, :], Nre[:, k - 1, :], Nim[:, k - 1, :], Nre[:, 1, :], Nim[:, 1, :], t4, t5)
    ts_(t2, Ere[:, 1, :], -1.0, None, ALU.add)
    tt_(t3, lr, lr, ALU.mult)
    tt_(t4, li, li, ALU.mult)
    tt_(t3, t3, t4, ALU.add)
    dve(lambda e: e.reciprocal(out=t3, in_=t3))
    tt_(t4, t2, lr, ALU.mult)
    tt_(t5, Eim[:, 1, :], li, ALU.mult)
    tt_(t4, t4, t5, ALU.add)
    tt_(fre, t4, t3, ALU.mult)
    tt_(t4, Eim[:, 1, :], lr, ALU.mult)
    tt_(t5, t2, li, ALU.mult)
    tt_(t4, t4, t5, ALU.subtract)
    tt_(fim, t4, t3, ALU.mult)
    P.barrier()
    if STOP == 'S':
        P.emit(); return nc

    off = PERS_END
    dsk, off = carve(off, [256], F32)
    WstT = []
    for i in range(2):
        ap, off = carve(off, [2 * NPR, 128], BF16); WstT.append(ap)
    Tbf, off = carve(off, [2 * GB, 128], BF16)
    CwB = []
    for i in range(2):
        ap, off = carve(off, [2 * NPR, 2, 128], BF16); CwB.append(ap)
    SRe, off = carve(off, [2 * NPR, NCH + 1], F32)
    SIm, off = carve(off, [2 * NPR, NCH + 1], F32)
    sbR, off = carve(off, [2 * NPR, NCH + 2], BF16)
    sbI, off = carve(off, [2 * NPR, NCH + 2], BF16)
    Ug, off = carve(off, [GB, NCH], BF16)
    XB = off
    Craw = []
    Cw = []
    for i in range(2):
        ap2, _ = carve(off, [2 * NPR, 8, 16], F32); Cw.append(ap2)
        ap, off = carve(off, [2, NPR, 128], F32, parts=16); Craw.append(ap)
    Ct = []
    for i in range(2):
        ap, off = carve(off, [2 * NPR, 16], F32); Ct.append(ap)
    Braw = []
    for i in range(2):
        ap, off = carve(off, [2 * NPR, 16], F32); Braw.append(ap)
    Bb = []
    for i in range(2):
        ap, off = carve(off, [2 * NPR, 16], F32); Bb.append(ap)
    Vv = []
    for i in range(2):
        ap, off = carve(off, [2 * NPR, 8, 16], F32); Vv.append(ap)
    Rr = []
    for i in range(2):
        ap, off = carve(off, [2 * NPR, 8, 16], F32); Rr.append(ap)
    tA, off = carve(off, [NPR, 16], F32)
    tB, off = carve(off, [NPR, 16], F32)
    off = XB
    U2f = []
    for i in range(3):
        ap, off = carve(off, [GB, 128], F32); U2f.append(ap)
    Y2, off = carve(off, [2, GB, 128], F32)
    Gt, off = carve(off, [GB, 128], F32)
    Zb, off = carve(off, [8, 256], BF16)
    ApR, off = carve(off, [2 * NPR, 16], F32); ApI, off = carve(off, [2 * NPR, 16], F32)
    apt = []
    for i in range(2):
        ap, off = carve(off, [2 * NPR], F32); apt.append(ap)
    sct = []
    for d_ in range(2):
        row = []
        for q in range(4):
            ap, off = carve(off, [NPR, 16], F32); row.append(ap)
        sct.append(row)

    R_B = Res("blk"); R_Ug = Res("Ug"); R_U2f = Res("U2f"); R_SS = Res("SS"); R_sb = Res("sb"); R_Y2 = Res("Y2")
    R_W = Res("blkW"); R_Z = Res("Zb"); R_Zscr = Res("Zscr"); R_sf = Res("sf"); R_sbw = Res("sbw")
    ctile = [(0, 32), (32, 128), (160, 128)]
    GC2 = 2.0 * math.sqrt(2.0 / math.pi)

    def dveB(fn, reads=(R_B,), writes=(R_B,)):
        P.op("dve", fn, reads=list(reads) + [R_S], writes=list(writes))

    for i in range(2):
        P.op("dve", lambda e, i=i: e.memset(CwB[i][:], 0.0), writes=[R_W])
    for blk in range(NBLK):
        g0 = blk * GB
        pr0 = blk * NPR
        for ri, (cc, bbr) in enumerate(((c_re, b_re), (c_im, b_im))):
            for d_ in range(2):
                for g2 in range(2):
                    sdma(Craw[ri][:, d_, :, g2 * 64:(g2 + 1) * 64],
                         cc[0, d_, g0:g0 + GB].rearrange("(pr g2) h p -> g2 h pr p", g2=2)[g2], [R_B])
            for g2 in range(2):
                sl = slice(g2 * 64, (g2 + 1) * 64)
                for d_ in range(2):
                    sdma(Braw[ri][sl, d_ * NPR:(d_ + 1) * NPR, :],
                         bbr[0, d_, g0:g0 + GB].rearrange("(pr g2) p h -> g2 p pr h", g2=2)[g2], [R_B])
        sdma(dsk, d_skip[0, g0 * 16:g0 * 16 + 256].partition_broadcast(128), [R_B])
        pb = 6
        pv = pbank[pb][:, 0:2 * NPR * 16].rearrange("p (a b) -> p a b", a=2 * NPR)
        pv2 = pbank[pb][:, 256:256 + 2 * NPR * 16].rearrange("p (a b) -> p a b", a=2 * NPR)
        for ri, pvv in ((0, pv), (1, pv2)):
            for d in range(2):
                for pr in range(NPR):
                    P.op("pe", lambda e, ri=ri, d=d, pr=pr, pvv=pvv: e.transpose(out=pvv[:, d * NPR + pr, :], in_=Craw[ri][0:16, d, pr, :],
                                                                               identity=identf[0:16, 0:16]),
                         reads=[R_B, R_const], writes=[R_pb[pb]])
        for ri, pvv in ((0, pv), (1, pv2)):
            dveB(lambda e, ri=ri, pvv=pvv: e.tensor_copy(out=Ct[ri][:], in_=pvv), reads=[R_B, R_pb[pb]], writes=[R_B])

        def cmulB(ore, oim, are, aim, bre, bim, shape_t):
            ta, tb = shape_t
            dveB(lambda e: e.tensor_tensor(out=ta, in0=aim, in1=bim, op=ALU.mult))
            dveB(lambda e: e.tensor_tensor(out=tb, in0=aim, in1=bre, op=ALU.mult))
            dveB(lambda e: e.tensor_tensor(out=ore, in0=are, in1=bre, op=ALU.mult))
            dveB(lambda e: e.tensor_tensor(out=ore, in0=ore, in1=ta, op=ALU.subtract))
            dveB(lambda e: e.tensor_tensor(out=oim, in0=are, in1=bim, op=ALU.mult))
            dveB(lambda e: e.tensor_tensor(out=oim, in0=oim, in1=tb, op=ALU.add))

        for d in range(2):
            cs_ = slice(d * 128 + pr0, d * 128 + pr0 + NPR)
            ps_ = slice(d * NPR, (d + 1) * NPR)
            def bc(ap2):
                return ap2.unsqueeze(2).broadcast_to([128, NPR, 16])
            cmulB(Bb[0][:, ps_, :], Bb[1][:, ps_, :], bc(fre[:, cs_]), bc(fim[:, cs_]), Braw[0][:, ps_, :], Braw[1][:, ps_, :], (tA, tB))
            for j in range(8):
                ej = 7 - j if d == 0 else j
                cmulB(Vv[0][:, ps_, j, :], Vv[1][:, ps_, j, :], bc(Ere[:, ej, cs_]), bc(Eim[:, ej, cs_]), Bb[0][:, ps_, :], Bb[1][:, ps_, :], (tA, tB))
                ni = 7 - j if d == 0 else j
                cmulB(Rr[0][:, ps_, j, :], Rr[1][:, ps_, j, :], bc(Nre[:, ni, cs_]), bc(Nim[:, ni, cs_]), Ct[0][:, ps_, :], Ct[1][:, ps_, :], (tA, tB))
                ci = j + 1 if d == 0 else 8 - j
                cmulB(Cw[0][:, ps_, j, :], Cw[1][:, ps_, j, :], bc(Ere[:, ci, cs_]), bc(Eim[:, ci, cs_]), Ct[0][:, ps_, :], Ct[1][:, ps_, :], (tA, tB))
        dveB(lambda e: e.tensor_scalar(out=Rr[1][:], in0=Rr[1][:], scalar1=-1.0, scalar2=None, op0=ALU.mult, op1=ALU.bypass))
        for g2 in range(2):
            sl = slice(g2 * 64, g2 * 64 + 64)
            dveB(lambda e, sl=sl, g2=g2: e.tensor_scalar(out=CwB[1][sl, :, g2, :], in0=Cw[1][sl, :, :, :].rearrange("p a b c -> p a (b c)"),
                                                         scalar1=-1.0, scalar2=None, op0=ALU.mult, op1=ALU.bypass), writes=[R_B, R_W])
            dveB(lambda e, sl=sl, g2=g2: e.tensor_copy(out=CwB[0][sl, :, g2, :], in_=Cw[0][sl, :, :, :].rearrange("p a b c -> p a (b c)")),
                 writes=[R_B, R_W])
        for ri in range(2):
            for pd in range(2 * NPR):
                pb = 6 + (pd % 2)
                P.op("pe", lambda e, ri=ri, pd=pd, pb=pb: e.transpose(out=pbank[pb][:, 0:128], in_=Vv[ri][:, pd, :, :].rearrange("p a b -> p (a b)"),
                                                                     identity=identf[:]),
                     reads=[R_B, R_const], writes=[R_pb[pb]])
                P.op("act", lambda e, ri=ri, pd=pd, pb=pb: e.activation(out=WstT[ri][:, pd, :], in_=pbank[pb][:, 0:128], func=AF.Copy),
                     reads=[R_pb[pb]], writes=[R_W])
        for d in range(2):
            for gl in range(GB):
                pd = d * NPR + gl // 2
                sl = slice((gl % 2) * 64, (gl % 2) * 64 + 64)
                pb = 6 + (gl % 2)
                P.op("pe", lambda e, pd=pd, sl=sl, pb=pb: e.matmul(out=pbank[pb][:, 0:128], lhsT=Vv[0][sl, pd, :, :].rearrange("p a b -> p (a b)"),
                                                                  rhs=Rr[0][sl, pd, :, :].rearrange("p a b -> p (a b)"), start=True, stop=False),
                     reads=[R_B], writes=[R_pb[pb]])
                P.op("pe", lambda e, pd=pd, sl=sl, pb=pb: e.matmul(out=pbank[pb][:, 0:128], lhsT=Vv[1][sl, pd, :, :].rearrange("p a b -> p (a b)"),
                                                                  rhs=Rr[1][sl, pd, :, :].rearrange("p a b -> p (a b)"), start=False, stop=True),
                     reads=[R_B], writes=[R_pb[pb]])
                P.op("dve", lambda e, d=d, gl=gl, pb=pb: e.tensor_tensor(out=Tbf[:, d * GB + gl, :], in0=pbank[pb][:, 0:128], in1=maskf[:, d, :], op=ALU.mult),
                     reads=[R_pb[pb], R_const], writes=[R_W])
        P.barrier()
        for ct, (cb, cn) in enumerate(ctile):
            P.dma("sp", lambda e, ct=ct, cb=cb, cn=cn, g0=g0: e.dma_start(out=U2f[ct][0:cn, :, :], in_=Uscr[cb:cb + cn, g0:g0 + GB, :]),
                  reads=[R_Uscr], writes=[R_U2f])
        tcnt = 0
        for gl in range(GB):
            for ct, (cb, cn) in enumerate(ctile):
                pb = 4 + (tcnt % 2)
                tcnt += 1
                P.op("pe", lambda e, gl=gl, ct=ct, cn=cn, pb=pb: e.transpose(out=pbank[pb][:, 0:cn], in_=U2f[ct][0:cn, gl, :],
                                                                           identity=identf[0:cn, 0:cn]),
                     reads=[R_U2f, R_const], writes=[R_pb[pb]])
                eng = "act" if tcnt % 2 == 0 else "dve"
                if eng == "act":
                    P.op("act", lambda e, gl=gl, cb=cb, cn=cn, pb=pb: e.activation(out=Ug[:, gl, cb:cb + cn], in_=pbank[pb][:, 0:cn], func=AF.Copy),
                         reads=[R_pb[pb]], writes=[R_Ug])
                else:
                    P.op("dve", lambda e, gl=gl, cb=cb, cn=cn, pb=pb: e.tensor_copy(out=Ug[:, gl, cb:cb + cn], in_=pbank[pb][:, 0:cn]),
                         reads=[R_pb[pb]], writes=[R_Ug])
        for pd in range(2 * NPR):
            pr = pd % NPR
            for ri, Sdst in ((0, SRe), (1, SIm)):
                pb = 6 + ri
                for g2 in range(2):
                    sl = slice(g2 * 64, g2 * 64 + 64)
                    P.op("pe", lambda e, ri=ri, pd=pd, sl=sl, pb=pb, gl=2 * pr + g2: e.matmul(
                        out=pbank[pb][sl, 0:NCH], lhsT=WstT[ri][:, pd, sl], rhs=Ug[:, gl, :], start=True, stop=True),
                        reads=[R_W, R_Ug], writes=[R_pb[pb]])
                if ri == 0:
                    P.op("act", lambda e, pd=pd, pb=pb, Sdst=Sdst: e.activation(out=Sdst[:, pd, 0:NCH], in_=pbank[pb][:, 0:NCH], func=AF.Copy),
                         reads=[R_pb[pb]], writes=[R_SS])
                else:
                    P.op("dve", lambda e, pd=pd, pb=pb, Sdst=Sdst: e.tensor_copy(out=Sdst[:, pd, 0:NCH], in_=pbank[pb][:, 0:NCH]),
                         reads=[R_pb[pb]], writes=[R_SS])
        R_ap = Res("apow")
        for d in range(2):
            cs_ = slice(d * 128 + pr0, d * 128 + pr0 + NPR)
            ps_ = slice(d * NPR, (d + 1) * NPR)
            P.op("dve", lambda e, cs_=cs_, ps_=ps_: e.tensor_copy(out=ApR[:, ps_, 0], in_=Ere[:, 8, cs_]), reads=[R_S, R_ap], writes=[R_ap])
            P.op("dve", lambda e, cs_=cs_, ps_=ps_: e.tensor_copy(out=ApI[:, ps_, 0], in_=Eim[:, 8, cs_]), reads=[R_S, R_ap], writes=[R_ap])
        for k in range(1, 16):
            def apo(fn):
                P.op("dve", fn, reads=[R_ap], writes=[R_ap])
            apo(lambda e, k=k: e.tensor_tensor(out=apt[0], in0=ApI[:, :, k - 1], in1=ApI[:, :, 0], op=ALU.mult))
            apo(lambda e, k=k: e.tensor_tensor(out=apt[1], in0=ApI[:, :, k - 1], in1=ApR[:, :, 0], op=ALU.mult))
            apo(lambda e, k=k: e.tensor_tensor(out=ApR[:, :, k], in0=ApR[:, :, k - 1], in1=ApR[:, :, 0], op=ALU.mult))
            apo(lambda e, k=k: e.tensor_tensor(out=ApR[:, :, k], in0=ApR[:, :, k], in1=apt[0], op=ALU.subtract))
            apo(lambda e, k=k: e.tensor_tensor(out=ApI[:, :, k], in0=ApR[:, :, k - 1], in1=ApI[:, :, 0], op=ALU.mult))
            apo(lambda e, k=k: e.tensor_tensor(out=ApI[:, :, k], in0=ApI[:, :, k], in1=apt[1], op=ALU.add))
        for d in range(2):
            ps_ = slice(d * NPR, (d + 1) * NPR)
            R_re = Res("scre%d" % d); R_im = Res("scim%d" % d); R_tq = [Res("sct%d_%d" % (d, q)) for q in range(4)]
            tq = sct[d]
            first = [True]

            def vw(S, start, stride, cnt):
                return S[:, ps_, start:start + stride * (cnt - 1) + 1:stride] if cnt > 1 else S[:, ps_, start:start + 1]

            def madd(ydst, xsrc, kpow, cnt):
                ar = ApR[:, ps_, kpow - 1:kpow].broadcast_to([128, NPR, cnt]) if cnt > 1 else ApR[:, ps_, kpow - 1:kpow]
                ai = ApI[:, ps_, kpow - 1:kpow].broadcast_to([128, NPR, cnt]) if cnt > 1 else ApI[:, ps_, kpow - 1:kpow]
                xr, xi = vw(SRe, xsrc[0], xsrc[1], cnt), vw(SIm, xsrc[0], xsrc[1], cnt)
                yr, yi = vw(SRe, ydst[0], ydst[1], cnt), vw(SIm, ydst[0], ydst[1], cnt)
                t = [tq[q][:, :, 0:cnt] for q in range(4)]
                ex = [R_SS, R_sf, R_sbw] if first[0] else []
                first[0] = False
                P.op("dve", lambda e: e.tensor_tensor(out=t[0], in0=ar, in1=xr, op=ALU.mult), reads=[R_re, R_ap] + ex, writes=[R_tq[0]])
                P.op("dve", lambda e: e.tensor_tensor(out=t[1], in0=ai, in1=xi, op=ALU.mult), reads=[R_im, R_ap] + ex, writes=[R_tq[1]])
                P.op("dve", lambda e: e.tensor_tensor(out=t[2], in0=ar, in1=xi, op=ALU.mult), reads=[R_im, R_ap], writes=[R_tq[2]])
                P.op("dve", lambda e: e.tensor_tensor(out=t[3], in0=ai, in1=xr, op=ALU.mult), reads=[R_re, R_ap], writes=[R_tq[3]])
                P.op("dve", lambda e: e.tensor_tensor(out=yr, in0=yr, in1=t[0], op=ALU.add), reads=[R_tq[0]], writes=[R_re])
                P.op("dve", lambda e: e.tensor_tensor(out=yr, in0=yr, in1=t[1], op=ALU.subtract), reads=[R_tq[1]], writes=[R_re])
                P.op("dve", lambda e: e.tensor_tensor(out=yi, in0=yi, in1=t[2], op=ALU.add), reads=[R_tq[2]], writes=[R_im])
                P.op("dve", lambda e: e.tensor_tensor(out=yi, in0=yi, in1=t[3], op=ALU.add), reads=[R_tq[3]], writes=[R_im])

            if d == 0:
                for w in range(1, 8):
                    madd((w, 8), (w - 1, 8), 1, 4)
                for g in range(1, 4):
                    madd((8 * g + 7, 1), (8 * g - 1, 1), 8, 1)
                for w in range(0, 7):
                    madd((8 + w, 8), (7, 8), w + 1, 3)
                for w in range(1, 16):
                    madd((32 + w, 16), (32 + w - 1, 16), 1, 16)
                for g in range(16):
                    madd((47 + 16 * g, 1), (31 + 16 * g, 1), 16, 1)
                for w in range(0, 15):
                    madd((32 + w, 16), (31, 16), w + 1, 16)
            else:
                for w in range(1, 8):
                    madd((7 - w, 8), (8 - w, 8), 1, 4)
                for g in (2, 1, 0):
                    madd((8 * g, 1), (8 * g + 8, 1), 8, 1)
                for w in range(0, 7):
                    madd((7 - w, 8), (8, 8), w + 1, 3)
                for Sx, Rx in ((SRe, R_re), (SIm, R_im)):
                    P.op("dve", lambda e, Sx=Sx, ps_=ps_: e.tensor_copy(out=Sx[:, ps_, NCH:NCH + 1], in_=Sx[:, ps_, 0:1]), reads=[Rx], writes=[Rx])
                for w in range(1, 16):
                    madd((47 - w, 16), (48 - w, 16), 1, 16)
                for g in range(15, -1, -1):
                    madd((32 + 16 * g, 1), (48 + 16 * g, 1), 16, 1)
                for w in range(0, 15):
                    madd((47 - w, 16), (48, 16), w + 1, 16)
            P.op("dve", lambda e, tq=tq: e.tensor_copy(out=tq[0][:, :, 0:1], in_=tq[0][:, :, 0:1]), reads=[R_re, R_im] + R_tq, writes=[R_tq[0], R_sf if d == 0 else R_sbw])
        for Ssrc, sdst in ((SRe, sbR), (SIm, sbI)):
            P.op("act", lambda e, Ssrc=Ssrc, sdst=sdst: e.activation(out=sdst[:, :, 1:NCH + 1], in_=Ssrc[:, :, 0:NCH], func=AF.Copy),
                 reads=[R_sf, R_sbw, R_SS], writes=[R_sb])
            P.op("act", lambda e, Ssrc=Ssrc, sdst=sdst: e.activation(out=sdst[:, :, NCH + 1:NCH + 2], in_=Ssrc[:, :, 0:1], func=AF.Copy),
                 reads=[R_sf, R_sbw, R_SS], writes=[R_sb])
        for lt in range(2):
            cb = 32 + lt * 128
            for gl in range(GB):
                pb = 4 + (gl % 2)
                sl = slice((gl % 2) * 64, (gl % 2) * 64 + 64)
                pr = gl // 2
                ops = []
                ops.append((Ug[:, gl, cb:cb + 128], Tbf[:, gl, :]))
                ops.append((Ug[:, gl, cb:cb + 128], Tbf[:, GB + gl, :]))
                g2 = gl % 2
                ops.append((sbR[:, pr, cb:cb + 128], CwB[0][:, pr, g2, :]))
                ops.append((sbI[:, pr, cb:cb + 128], CwB[1][:, pr, g2, :]))
                ops.append((sbR[:, NPR + pr, cb + 2:cb + 130], CwB[0][:, NPR + pr, g2, :]))
                ops.append((sbI[:, NPR + pr, cb + 2:cb + 130], CwB[1][:, NPR + pr, g2, :]))
                for oi, (l_, r_) in enumerate(ops):
                    P.op("pe", lambda e, l_=l_, r_=r_, oi=oi, pb=pb: e.matmul(out=pbank[pb][:, 0:128], lhsT=l_, rhs=r_, start=(oi == 0), stop=(oi == 5)),
                         reads=[R_Ug, R_W, R_sb], writes=[R_pb[pb]])
                P.op("act", lambda e, lt=lt, gl=gl, pb=pb: e.activation(out=Y2[:, lt, gl, :], in_=pbank[pb][:, 0:128], func=AF.Copy),
                     reads=[R_pb[pb]], writes=[R_Y2])
        dbc = dsk.rearrange("p (g h) -> p g h", g=GB).unsqueeze(2).broadcast_to([128, GB, 8, 16])
        v4 = lambda ap: ap.rearrange("p g (i h) -> p g i h", i=8)
        for lt in range(2):
            yv = v4(Y2[:, lt, :, :])
            uv = v4(U2f[1 + lt][:, :, :])
            gv = v4(Gt[:])
            rr = [R_Y2, R_U2f, R_B, R_Z]
            P.op("dve", lambda e, uv=uv: e.tensor_tensor(out=gv, in0=uv, in1=dbc, op=ALU.mult), reads=rr, writes=[R_Z])
            P.op("dve", lambda e, yv=yv: e.tensor_tensor(out=yv, in0=yv, in1=gv, op=ALU.add), reads=rr, writes=[R_Y2, R_Z])
            P.op("dve", lambda e, yv=yv: e.tensor_tensor(out=gv, in0=yv, in1=yv, op=ALU.mult), reads=rr, writes=[R_Z])
            P.op("dve", lambda e: e.tensor_scalar(out=gv, in0=gv, scalar1=0.044715, scalar2=1.0, op0=ALU.mult, op1=ALU.add), reads=rr, writes=[R_Z])
            P.op("dve", lambda e, yv=yv: e.tensor_tensor(out=gv, in0=gv, in1=yv, op=ALU.mult), reads=rr, writes=[R_Z])
            P.op("act", lambda e: e.activation(out=gv, in_=gv, func=AF.Sigmoid, scale=GC2), reads=rr, writes=[R_Z])
            P.op("dve", lambda e, yv=yv: e.tensor_tensor(out=Zb[:].rearrange("p i (g h) -> p g i h", g=GB), in0=yv, in1=gv, op=ALU.mult),
                 reads=rr, writes=[R_Z])
            P.dma("sp", lambda e, lt=lt, g0=g0: e.dma_start(out=Zscr[lt * 128:(lt + 1) * 128, :, g0 * 16:g0 * 16 + 256], in_=Zb[:]),
                  reads=[R_Z], writes=[R_Zscr], semres=R_Z)
        P.barrier()
        if STOP == '0B%d' % blk:
            P.emit(); return nc
    if STOP == '0B':
        P.emit(); return nc

    off = ARENA0
    xaT, off = carve(off, [KC, 512], F32)
    hT2, off = carve(off, [KC, 512], BF16)
    QG, off = carve(off, [KC, 512], BF16)
    Z2i = []
    zoff = off
    for i in range(2):
        ap, off = carve(off, [D], BF16, parts=64); Z2i.append(ap)
    O2, _ = carve(zoff, [D], F32, parts=64)
    cvt = []
    o2 = zoff
    for i in range(3):
        ap, o2 = carve(o2, [512], F32); cvt.append(ap)
    tmpf = []
    for i in range(4):
        ap, off = carve(off, [512], F32); tmpf.append(ap)
    meanS, off = carve(off, [512], F32); rstdS, off = carve(off, [512], F32); m2S, off = carve(off, [512], F32)
    R_xa = Res("xaT"); R_h = Res("hT2"); R_QG = Res("QG"); R_Z2 = Res("Z2i"); R_tmp = [Res("tmp%d" % i) for i in range(4)]
    R_st = Res("stats"); R_out = Res("out")
    mmc = [0]
    tcn = [0]

    def mm(cv, m_, k0, nk, rhs, rres, consume):
        wt, rw = load_w(cv, m_, k0, nk)
        pb = mmc[0] % 4
        mmc[0] += 1
        for k in range(nk):
            P.op("pe", lambda e, k=k, pb=pb, wt=wt: e.matmul(out=pbank[pb][:, :], lhsT=wt[:, k, :], rhs=rhs(k), start=(k == 0), stop=(k == nk - 1)),
                 reads=[rw] + rres, writes=[R_pb[pb]])
        consume(pb)

    def layer_norm(n, last):
        for k in range(KC):
            ti = tcn[0] % 4
            tcn[0] += 1
            P.op("act", lambda e, k=k, ti=ti: e.activation(out=tmpf[ti], in_=xaT[:, k, :], func=AF.Square), reads=[R_xa], writes=[R_tmp[ti]])
            P.op("pe", lambda e, k=k: e.matmul(out=pbank[4][:, :], lhsT=onesD[:], rhs=xaT[:, k, :], start=(k == 0), stop=(k == KC - 1)),
                 reads=[R_xa, R_const], writes=[R_pb[4]])
            P.op("pe", lambda e, k=k, ti=ti: e.matmul(out=pbank[5][:, :], lhsT=onesD[:], rhs=tmpf[ti], start=(k == 0), stop=(k == KC - 1)),
                 reads=[R_tmp[ti], R_const], writes=[R_pb[5]])
        P.op("dve", lambda e: e.tensor_copy(out=meanS, in_=pbank[4][:, :]), reads=[R_pb[4]], writes=[R_st])
        P.op("dve", lambda e: e.tensor_tensor(out=m2S, in0=meanS, in1=meanS, op=ALU.mult), reads=[R_st], writes=[R_st])
        P.op("dve", lambda e: e.tensor_tensor(out=m2S, in0=pbank[5][:, :], in1=m2S, op=ALU.subtract), reads=[R_st, R_pb[5]], writes=[R_st])
        P.op("dve", lambda e: e.tensor_scalar(out=m2S, in0=m2S, scalar1=EPS, scalar2=None, op0=ALU.add, op1=ALU.bypass), reads=[R_st], writes=[R_st])
        P.op("act", lambda e: e.activation(out=m2S, in_=m2S, func=AF.Sqrt), reads=[R_st], writes=[R_st])
        P.op("dve", lambda e: e.reciprocal(out=rstdS, in_=m2S), reads=[R_st], writes=[R_st])
        for k in range(KC):
            ti = tcn[0] % 4
            tcn[0] += 1
            P.op("dve", lambda e, k=k, ti=ti: e.tensor_tensor(out=tmpf[ti], in0=xaT[:, k, :], in1=meanS, op=ALU.subtract),
                 reads=[R_xa, R_st], writes=[R_tmp[ti]])
            P.op("dve", lambda e, k=k, ti=ti: e.scalar_tensor_tensor(out=tmpf[ti], in0=tmpf[ti], scalar=lng[:, n, k:k + 1], in1=rstdS,
                                                                    op0=ALU.mult, op1=ALU.mult), reads=[R_tmp[ti], R_st, R_mv], writes=[R_tmp[ti]])
            P.op("act", lambda e, k=k, ti=ti: e.activation(out=xaT[:, k, :], in_=tmpf[ti], func=AF.Identity,
                                                          bias=dv[:, 4 + n * 3, k:k + 1], scale=(1.0 if last else ALPHA)),
                 reads=[R_tmp[ti], R_mv], writes=[R_xa])
            if not last:
                P.op("act", lambda e, k=k, ti=ti: e.activation(out=hT2[:, k, :], in_=tmpf[ti], func=AF.Identity,
                                                              bias=dv[:, 6 + n * 3, k:k + 1], scale=dv[:, 5 + n * 3, k:k + 1]),
                     reads=[R_tmp[ti], R_mv], writes=[R_h])

    def resid(gr, gt_):
        def consume_factory(m):
            def consume(pb):
                P.op("dve", lambda e, m=m, pb=pb: e.scalar_tensor_tensor(out=xaT[:, m, :], in0=pbank[pb][:, :], scalar=mvraw[:, gr, gt_ * 32 + m:gt_ * 32 + m + 1],
                                                                      in1=xaT[:, m, :], op0=ALU.mult, op1=ALU.add),
                     reads=[R_pb[pb], R_xa, R_mv], writes=[R_xa])
            return consume
        return consume_factory

    def ffn(l, gr):
        parts = [(0, 30), (30, 28), (58, 28)]
        for (j0, nj) in parts:
            for jj in range(nj):
                j = j0 + jj
                hold = {}
                def c1(pb, jj=jj):
                    ti = tcn[0] % 4
                    tcn[0] += 1
                    hold["ti"] = ti
                    P.op("act", lambda e, pb=pb, ti=ti: e.activation(out=tmpf[ti], in_=pbank[pb][:, :], func=AF.Silu), reads=[R_pb[pb]], writes=[R_tmp[ti]])
                def c3(pb, jj=jj):
                    ti = hold["ti"]
                    P.op("dve", lambda e, pb=pb, ti=ti, jj=jj: e.tensor_tensor(out=QG[:, jj, :], in0=pbank[pb][:, :], in1=tmpf[ti], op=ALU.mult),
                         reads=[R_pb[pb], R_tmp[ti]], writes=[R_QG])
                mm(CV["w1_%d" % l], j, 0, KC, lambda k: hT2[:, k, :], [R_h], c1)
                mm(CV["w3_%d" % l], j, 0, KC, lambda k: hT2[:, k, :], [R_h], c3)
            cf = resid(gr, 5)
            for m in range(KC):
                mm(CV["w2_%d" % l], m, j0, nj, lambda k: QG[:, k, :], [R_QG], cf(m))

    for tt in range(NT):
        P.dma("sp", lambda e, tt=tt: e.dma_start(out=xaT[:].rearrange("p a b -> p (a b)"), in_=XAscr[tt], max_dma_last_dim=16384), reads=[R_XAscr], writes=[R_xa])
        zc = 0
        for i in range(8):
            zb = zc % 2
            zc += 1
            P.dma("sp", lambda e, zb=zb, i=i, tt=tt: e.dma_start(out=Z2i[zb][0:64, :], in_=Zscr[tt * 64:(tt + 1) * 64, i, :]),
                  reads=[R_Zscr], writes=[R_Z2])
            for k8 in range(4):
                pb = 6 + (k8 % 2)
                pzb = pbank[pb][:, 0:256].bitcast(BF16).rearrange("p (a b) -> p a b", a=8)
                for kk in range(8):
                    k = k8 * 8 + kk
                    P.op("pe", lambda e, pzb=pzb, kk=kk, k=k, zb=zb: e.transpose(out=pzb[:, kk, :], in_=Z2i[zb][0:64, k * 128:(k + 1) * 128],
                                                                              identity=identb[0:64, 0:64]),
                         reads=[R_Z2, R_const], writes=[R_pb[pb]])
                P.op("dve", lambda e, pzb=pzb, k8=k8, i=i: e.tensor_copy(out=hT2[:, k8 * 8:(k8 + 1) * 8, i * 64:(i + 1) * 64], in_=pzb),
                     reads=[R_pb[pb]], writes=[R_h])
        for m in range(KC):
            def cg(pb, m=m):
                ti = tcn[0] % 4
                tcn[0] += 1
                P.op("act", lambda e, pb=pb, ti=ti: e.activation(out=tmpf[ti], in_=pbank[pb][:, :], func=AF.Sigmoid), reads=[R_pb[pb]], writes=[R_tmp[ti]])
                P.op("dve", lambda e, ti=ti, m=m: e.tensor_tensor(out=QG[:, m, :], in0=hT2[:, m, :], in1=tmpf[ti], op=ALU.mult),
                     reads=[R_tmp[ti], R_h], writes=[R_QG])
            mm(CV["w_glu_a"], m, 0, KC, lambda k: hT2[:, k, :], [R_h], cg)
        cf = resid(0, 2)
        for m in range(KC):
            mm(CV["w_out_a"], m, 0, KC, lambda k: QG[:, k, :], [R_QG], cf(m))
        layer_norm(0, False)
        ffn(0, 0)
        layer_norm(1, False)
        for m in range(KC):
            def cgc(pb):
                P.op("act", lambda e, pb=pb: e.activation(out=cvt[0], in_=pbank[pb][:, :], func=AF.Copy), reads=[R_pb[pb]], writes=[R_Z2])
            def cv(pb, m=m):
                P.op("dve", lambda e, pb=pb: e.tensor_tensor(out=cvt[1], in0=pbank[pb][:, :], in1=cvt[0], op=ALU.mult), reads=[R_pb[pb], R_Z2], writes=[R_Z2])
                zz = cvt[1]
                cz = cvt[2]
                w0, w1_, w2_ = cwv[:, 0, m:m + 1], cwv[:, 1, m:m + 1], cwv[:, 2, m:m + 1]
                rr = [R_Z2, R_mv]
                P.op("dve", lambda e: e.tensor_scalar(out=cz, in0=zz, scalar1=w1_, scalar2=None, op0=ALU.mult, op1=ALU.bypass), reads=rr, writes=[R_Z2])
                P.op("dve", lambda e: e.scalar_tensor_tensor(out=cz[:, 64:512], in0=zz[:, 0:448], scalar=w0, in1=cz[:, 64:512], op0=ALU.mult, op1=ALU.add),
                     reads=rr, writes=[R_Z2])
                czv = cz[:, 0:64].rearrange("p (r c) -> p r c", r=8)
                zzv = zz[:, 448:512].rearrange("p (r c) -> p r c", r=8)
                P.op("dve", lambda e: e.scalar_tensor_tensor(out=czv[:, :, 1:8], in0=zzv[:, :, 0:7], scalar=w0, in1=czv[:, :, 1:8], op0=ALU.mult, op1=ALU.add),
                     reads=rr, writes=[R_Z2])
                P.op("dve", lambda e: e.scalar_tensor_tensor(out=cz[:, 0:448], in0=zz[:, 64:512], scalar=w2_, in1=cz[:, 0:448], op0=ALU.mult, op1=ALU.add),
                     reads=rr, writes=[R_Z2])
                czv2 = cz[:, 448:512].rearrange("p (r c) -> p r c", r=8)
                zzv2 = zz[:, 0:64].rearrange("p (r c) -> p r c", r=8)
                P.op("dve", lambda e: e.scalar_tensor_tensor(out=czv2[:, :, 0:7], in0=zzv2[:, :, 1:8], scalar=w2_, in1=czv2[:, :, 0:7], op0=ALU.mult, op1=ALU.add),
                     reads=rr, writes=[R_Z2])
            def cgb(pb, m=m):
                P.op("dve", lambda e, pb=pb, m=m: e.tensor_tensor(out=QG[:, m, :], in0=pbank[pb][:, :], in1=cvt[2], op=ALU.mult),
                     reads=[R_pb[pb], R_Z2], writes=[R_QG])
            mm(CV["w_in_b"], 32 + m, 0, KC, lambda k: hT2[:, k, :], [R_h], cgc)
            mm(CV["w_in_b"], 64 + m, 0, KC, lambda k: hT2[:, k, :], [R_h], cv)
            mm(CV["w_in_b"], m, 0, KC, lambda k: hT2[:, k, :], [R_h], cgb)
        cf = resid(2, 2)
        for m in range(KC):
            mm(CV["w_out_b"], m, 0, KC, lambda k: QG[:, k, :], [R_QG], cf(m))
        layer_norm(2, False)
        ffn(1, 2)
        layer_norm(3, True)
        orows = out[0, tt * 512:(tt + 1) * 512, :].rearrange("(cl i) d -> i cl d", i=8)
        for i in range(8):
            for k4 in range(8):
                pb = 6 + (k4 % 2)
                for kk in range(4):
                    k = k4 * 4 + kk
                    P.op("pe", lambda e, pb=pb, kk=kk, k=k, i=i: e.transpose(out=pbank[pb][0:64, kk * 128:(kk + 1) * 128], in_=xaT[:, k, i * 64:(i + 1) * 64],
                                                                           identity=identf[:]),
                         reads=[R_xa, R_const], writes=[R_pb[pb]])
                P.op("act", lambda e, pb=pb, k4=k4: e.activation(out=O2[0:64, k4 * 512:(k4 + 1) * 512], in_=pbank[pb][0:64, :], func=AF.Copy),
                     reads=[R_pb[pb]], writes=[R_Z2])
            P.dma("sp", lambda e, i=i, orows=orows: e.dma_start(out=orows[i], in_=O2[0:64, :]), reads=[R_Z2], writes=[R_out], semres=R_Z2)
    P.emit()
    return nc


_NC = [None]


def kernel(**inputs):
    n = 8
    if _NC[0] is None:
        _NC[0] = build()
    nc = _NC[0]
    in_maps = []
    shared = {k: np.ascontiguousarray(v) for k, v in inputs.items() if k not in ("x", "c", "ctx")}
    for b in range(n):
        m = dict(shared)
        m["x"] = np.ascontiguousarray(inputs["x"][b:b + 1])
        m["c"] = np.ascontiguousarray(inputs["c"][b:b + 1])
        m["ctx"] = np.ascontiguousarray(inputs["ctx"][b:b + 1])
        in_maps.append(m)
    res = run_bass_kernel_spmd(nc, in_maps, core_ids=list(range(n)))
    return np.concatenate([r["out"] for r in res.results], axis=0).astype(np.float32)
```
